# Optimizing a Trainium2 kernel written in Bass

```python
import math
import jax, jax.numpy as jnp
from jax import lax
import numpy as np

D_MODEL = 1024
BATCH = 8
SEQ = 8192
DEPTH = 2
DEC_BATCH = 8
DEC_SEQ = 4096
PAST_LEN = 128

ATT_GROUPS = ((128, 1), (512, 4), (2048, 16))
ATT_HEADS = 8
ATT_HEAD_DIM = 64
ATT_GROUP_WIDTH = ATT_HEADS * ATT_HEAD_DIM
ATT_QKV_WIDTH = len(ATT_GROUPS) * 3 * ATT_GROUP_WIDTH
ROPE_DIM = ATT_HEAD_DIM // 4
ROPE_THETA = 500000.0
MLSTM_HEADS = 4
MLSTM_QK_DIM = D_MODEL // (2 * MLSTM_HEADS)
MLSTM_V_DIM = D_MODEL // MLSTM_HEADS
MLSTM_CHUNK = 128
MLSTM_IN_WIDTH = 2 * MLSTM_HEADS * MLSTM_QK_DIM + 2 * MLSTM_HEADS * MLSTM_V_DIM + 4 * MLSTM_HEADS
FFN_HIDDEN = int(math.ceil(8 * D_MODEL / 3 / 256)) * 256
N_MIXERS = 2
RMS_EPS = 1e-6

kernel_name = "hybrid_dilated_attn_mlstm_encoder"


def rms_norm(x, g):
    xf = x.astype(jnp.float32)
    y = xf * lax.rsqrt(jnp.mean(xf * xf, axis=-1, keepdims=True) + RMS_EPS)
    return (y * g.astype(jnp.float32)).astype(x.dtype)


def rope_partial(x, pos):
    half = ROPE_DIM // 2
    inv = jnp.float32(ROPE_THETA) ** (-jnp.arange(half, dtype=jnp.float32) / half)
    ang = pos.astype(jnp.float32)[:, None] * inv[None, :]
    cos = jnp.cos(ang)[None, :, None, :]
    sin = jnp.sin(ang)[None, :, None, :]
    xr = x[..., :ROPE_DIM].astype(jnp.float32)
    x1, x2 = xr[..., :half], xr[..., half:]
    rot = jnp.concatenate([x1 * cos - x2 * sin, x2 * cos + x1 * sin], axis=-1).astype(x.dtype)
    return jnp.concatenate([rot, x[..., ROPE_DIM:]], axis=-1)


def dilated_window_attention(q, k, v, window, dilation):
    B, S, H, Dh = q.shape
    r = window // (2 * dilation)
    L = S // dilation
    nb = -(-L // r)
    Lp = nb * r

    def phases(t):
        t = t.reshape(B, L, dilation, H, Dh).transpose(0, 2, 1, 3, 4)
        return jnp.pad(t, ((0, 0), (0, 0), (0, Lp - L), (0, 0), (0, 0)))

    qb = phases(q).reshape(B, dilation, nb, r, H, Dh)
    pad = ((0, 0), (0, 0), (r, r), (0, 0), (0, 0))
    kp = jnp.pad(phases(k), pad)
    vp = jnp.pad(phases(v), pad)
    kb = jnp.concatenate([kp[:, :, j * r:j * r + Lp].reshape(B, dilation, nb, r, H, Dh) for j in range(3)], axis=3)
    vb = jnp.concatenate([vp[:, :, j * r:j * r + Lp].reshape(B, dilation, nb, r, H, Dh) for j in range(3)], axis=3)

    blk = jnp.arange(nb)[:, None, None]
    qi = jnp.arange(r)[None, :, None]
    kt = jnp.arange(3 * r)[None, None, :]
    kpos = blk * r - r + kt
    rel = kt - r - qi
    mask = (jnp.abs(rel) <= r) & (kpos >= 0) & (kpos < L)

    s = jnp.einsum('bpnqhd,bpnkhd->bpnhqk', qb, kb).astype(jnp.float32) * (Dh ** -0.5)
    s = jnp.where(mask[None, None, :, None], s, -jnp.inf)
    m = jnp.max(s, axis=-1, keepdims=True)
    p = jnp.exp(s - m)
    den = jnp.sum(p, axis=-1)
    o = jnp.einsum('bpnhqk,bpnkhd->bpnqhd', p, vb.astype(jnp.float32))
    o = o / jnp.transpose(den, (0, 1, 2, 4, 3))[..., None]
    lse = m[..., 0] + jnp.log(den)
    o = o.reshape(B, dilation, Lp, H, Dh)[:, :, :L].transpose(0, 2, 1, 3, 4).reshape(B, S, H, Dh)
    lse = jnp.transpose(lse, (0, 1, 2, 4, 3)).reshape(B, dilation, Lp, H)[:, :, :L]
    lse = lse.transpose(0, 2, 1, 3).reshape(B, S, H)
    return o, lse


def dilated_attention_mixer(h, w_qkv, w_o):
    B, S, _ = h.shape
    proj = (h @ w_qkv).reshape(B, S, len(ATT_GROUPS), 3, ATT_HEADS, ATT_HEAD_DIM)
    pos = jnp.arange(S)
    outs, lses = [], []
    for g, (window, dil) in enumerate(ATT_GROUPS):
        q = rope_partial(proj[:, :, g, 0], pos)
        k = rope_partial(proj[:, :, g, 1], pos)
        v = proj[:, :, g, 2]
        o, lse = dilated_window_attention(q, k, v, window, dil)
        outs.append(o)
        lses.append(lse)
    wts = jax.nn.softmax(jnp.stack(lses, axis=0), axis=0)
    y = jnp.sum(wts[..., None] * jnp.stack(outs, axis=0), axis=0).astype(h.dtype)
    return y.reshape(B, S, ATT_GROUP_WIDTH) @ w_o


def mlstm_scan(q, k, v, i_pre, logf):
    B, S, H, dk = q.shape
    dv = v.shape[-1]
    L = MLSTM_CHUNK
    nc = S // L

    def vec_chunks(t):
        return t.reshape(B, nc, L, H, t.shape[-1]).transpose(1, 0, 3, 2, 4)

    def gate_chunks(t):
        return t.reshape(B, nc, L, H).transpose(1, 0, 3, 2)

    tril = jnp.tril(jnp.ones((L, L), dtype=bool))

    def body(carry, xs):
        C, n, m = carry
        qc, kc, vc, ic, fc = xs
        b = jnp.cumsum(fc, axis=-1)
        D = b[..., :, None] - b[..., None, :] + ic[..., None, :]
        D = jnp.where(tril, D, -jnp.inf)
        m_inter = b + m[..., None]
        m_t = jnp.maximum(m_inter, jnp.max(D, axis=-1))
        wts = jnp.exp(D - m_t[..., None])
        inter = jnp.exp(m_inter - m_t)
        sqk = jnp.einsum('bhtd,bhsd->bhts', qc, kc) * wts
        num = jnp.einsum('bhts,bhsv->bhtv', sqk, vc) + inter[..., None] * jnp.einsum('bhtd,bhdv->bhtv', qc, C)
        den = jnp.sum(sqk, axis=-1) + inter * jnp.einsum('bhtd,bhd->bht', qc, n)
        hc = num / jnp.maximum(jnp.abs(den), jnp.exp(-m_t))[..., None]
        g = b[..., -1:] - b + ic
        m_new = jnp.maximum(b[..., -1] + m, jnp.max(g, axis=-1))
        ws = jnp.exp(g - m_new[..., None])
        decay = jnp.exp(b[..., -1] + m - m_new)
        C_new = decay[..., None, None] * C + jnp.einsum('bhs,bhsd,bhsv->bhdv', ws, kc, vc)
        n_new = decay[..., None] * n + jnp.einsum('bhs,bhsd->bhd', ws, kc)
        return (C_new, n_new, m_new), hc

    init = (jnp.zeros((B, H, dk, dv), jnp.float32), jnp.zeros((B, H, dk), jnp.float32),
            jnp.zeros((B, H), jnp.float32))
    xs = (vec_chunks(q), vec_chunks(k), vec_chunks(v), gate_chunks(i_pre), gate_chunks(logf))
    _, hs = lax.scan(body, init, xs)
    return hs.transpose(1, 0, 3, 2, 4).reshape(B, S, H, dv)


def mlstm_mixer(h, w_in, b_gates, head_norm, w_out):
    B, S, _ = h.shape
    H, dk, dv = MLSTM_HEADS, MLSTM_QK_DIM, MLSTM_V_DIM
    proj = h @ w_in
    cuts = np.cumsum([H * dk, H * dk, H * dv, H * dv]).tolist()
    q, k, v, o, gates = jnp.split(proj, cuts, axis=-1)
    q = q.reshape(B, S, H, dk).astype(jnp.float32)
    k = k.reshape(B, S, H, dk).astype(jnp.float32) * (dk ** -0.5)
    v = v.reshape(B, S, H, dv).astype(jnp.float32)
    gates = (gates.astype(jnp.float32) + b_gates.astype(jnp.float32)).reshape(B, S, 4, H)
    i_fw, logf_fw = gates[:, :, 0], jax.nn.log_sigmoid(gates[:, :, 1])
    i_bw, logf_bw = gates[:, :, 2], jax.nn.log_sigmoid(gates[:, :, 3])
    flip = lambda t: jnp.flip(t, axis=1)
    h_fw = mlstm_scan(q, k, v, i_fw, logf_fw)
    h_bw = flip(mlstm_scan(flip(q), flip(k), flip(v), flip(i_bw), flip(logf_bw)))
    hs = h_fw + h_bw
    hs = hs * lax.rsqrt(jnp.mean(hs * hs, axis=-1, keepdims=True) + RMS_EPS)
    hs = hs * head_norm.astype(jnp.float32).reshape(H, dv)
    y = jax.nn.sigmoid(o) * hs.reshape(B, S, H * dv).astype(h.dtype)
    return y @ w_out


def swiglu(h, w_gu, w_down):
    a, b = jnp.split(h @ w_gu, 2, axis=-1)
    return (jax.nn.silu(a) * b) @ w_down


def run_trunk(x, c, layers, mixers, final_norm):
    for i in range(DEPTH):
        ada_w, ada_b, g1, g2, w_gu, w_down = layers[i]
        mixer = mixers[i % N_MIXERS]
        mod = (jax.nn.silu(c) @ ada_w + ada_b)[:, None, :]
        sh1, sc1, gt1, sh2, sc2, gt2 = jnp.split(mod, 6, axis=-1)
        hmix = rms_norm(x, g1) * (1 + sc1) + sh1
        x = x + gt1 * mixer(hmix)
        hffn = rms_norm(x, g2) * (1 + sc2) + sh2
        x = x + gt2 * swiglu(hffn, w_gu, w_down)
    return rms_norm(x, final_norm)


def setup_inputs(seed: int = 0) -> dict:
    key = jax.random.key(seed)
    ks = iter(jax.random.split(key, 40))
    nrm = lambda shape, s=1.0: s * jax.random.normal(next(ks), shape, jnp.float32)
    dense = lambda fi, fo, s=1.0: nrm((fi, fo), s * fi ** -0.5)
    gain = lambda n: 1.0 + nrm((n,), 0.02)
    H = MLSTM_HEADS
    b_gates = jnp.concatenate([nrm((H,), 0.1), 3.0 + nrm((H,), 0.5), nrm((H,), 0.1), 3.0 + nrm((H,), 0.5)])
    return {
        "x_prompt": nrm((BATCH, SEQ, D_MODEL)),
        "x_sample": nrm((DEC_BATCH, DEC_SEQ, D_MODEL)),
        "c_prompt": nrm((BATCH, D_MODEL)),
        "c_sample": nrm((DEC_BATCH, D_MODEL)),
        "l0_ada_w": dense(D_MODEL, 6 * D_MODEL, 0.5),
        "l0_ada_b": nrm((6 * D_MODEL,), 0.02),
        "l0_norm1": gain(D_MODEL),
        "l0_attn_w_qkv": dense(D_MODEL, ATT_QKV_WIDTH),
        "l0_attn_w_o": dense(ATT_GROUP_WIDTH, D_MODEL),
        "l0_norm2": gain(D_MODEL),
        "l0_ffn_w_gu": dense(D_MODEL, 2 * FFN_HIDDEN),
        "l0_ffn_w_down": dense(FFN_HIDDEN, D_MODEL),
        "l1_ada_w": dense(D_MODEL, 6 * D_MODEL, 0.5),
        "l1_ada_b": nrm((6 * D_MODEL,), 0.02),
        "l1_norm1": gain(D_MODEL),
        "l1_mlstm_w_in": dense(D_MODEL, MLSTM_IN_WIDTH),
        "l1_mlstm_b_gates": b_gates,
        "l1_mlstm_head_norm": gain(MLSTM_HEADS * MLSTM_V_DIM),
        "l1_mlstm_w_out": dense(MLSTM_HEADS * MLSTM_V_DIM, D_MODEL),
        "l1_norm2": gain(D_MODEL),
        "l1_ffn_w_gu": dense(D_MODEL, 2 * FFN_HIDDEN),
        "l1_ffn_w_down": dense(FFN_HIDDEN, D_MODEL),
        "final_norm": gain(D_MODEL),
    }


def reference(x_prompt, x_sample, c_prompt, c_sample,
              l0_ada_w, l0_ada_b, l0_norm1, l0_attn_w_qkv, l0_attn_w_o, l0_norm2, l0_ffn_w_gu, l0_ffn_w_down,
              l1_ada_w, l1_ada_b, l1_norm1, l1_mlstm_w_in, l1_mlstm_b_gates, l1_mlstm_head_norm, l1_mlstm_w_out,
              l1_norm2, l1_ffn_w_gu, l1_ffn_w_down, final_norm):
    mixers = [
        lambda h: dilated_attention_mixer(h, l0_attn_w_qkv, l0_attn_w_o),
        lambda h: mlstm_mixer(h, l1_mlstm_w_in, l1_mlstm_b_gates, l1_mlstm_head_norm, l1_mlstm_w_out),
    ]
    layers = [
        (l0_ada_w, l0_ada_b, l0_norm1, l0_norm2, l0_ffn_w_gu, l0_ffn_w_down),
        (l1_ada_w, l1_ada_b, l1_norm1, l1_norm2, l1_ffn_w_gu, l1_ffn_w_down),
    ]
    y_prompt = run_trunk(x_prompt, c_prompt, layers, mixers, final_norm)
    y_sample = run_trunk(x_sample, c_sample, layers, mixers, final_norm)
    return (y_prompt, y_sample)
```

```python
import numpy as np
from contextlib import ExitStack
import concourse.bass as bass
import concourse.mybir as mybir
from concourse.bass_utils import run_bass_kernel_spmd

F32 = mybir.dt.float32
BF16 = mybir.dt.bfloat16
AF = mybir.ActivationFunctionType
ALU = mybir.AluOpType

D = 1024
NK = 8
TT = 512
FFN = 2816
NJF = FFN // 128
GROUPS = ((128, 1), (512, 4), (2048, 16))
EPS = 1e-6
NEG = -30000.0
STQ = "pool"


class Dep:
    __slots__ = ("w", "r")

    def __init__(self):
        self.w = None
        self.r = {}


class V:
    __slots__ = ("ap", "dep")

    def __init__(self, ap, dep):
        self.ap = ap
        self.dep = dep


class Tile(Dep):
    __slots__ = ("h",)

    def __init__(self, h):
        super().__init__()
        self.h = h

    def __getitem__(self, idx):
        return V(self.h[idx], self)

    def v(self, ap):
        return V(ap, self)


def _deps_of(xs):
    out = []
    for x in xs:
        if x is None or isinstance(x, (int, float)):
            continue
        out.append(x.dep if isinstance(x, V) else x)
    return out


def _ap(x):
    return x.ap if isinstance(x, V) else x


class Prog:
    CE = ("pe", "act", "dve", "pool")
    ALLE = ("pe", "act", "dve", "pool", "sp")

    def __init__(self, nc):
        self.nc = nc
        self.es = ExitStack()
        self.esem = {e: self.es.enter_context(nc.semaphore("c_" + e)) for e in self.CE}
        self.cnt = {e: 0 for e in self.CE}
        self.ring = {}
        self.rpos = {}
        self.ruse = {}
        for q, n in (("sp", 28), ("pool", 20), ("act", 2)):
            self.ring[q] = [self.es.enter_context(nc.semaphore(f"d_{q}{i}")) for i in range(n)]
            self.rpos[q] = 0
            for s_ in self.ring[q]:
                self.ruse[s_] = 0
        self.seen = {e: {} for e in self.ALLE}
        self.stream = {e: [] for e in self.ALLE}
        self.nops = 0

    def _waits(self, eng, reads, writes, is_dma):
        need = {}

        def add(tok):
            k, v_, _ = tok
            if need.get(k, 0) < v_:
                need[k] = v_

        for d in reads:
            if d.w is not None:
                add(d.w)
        for d in writes:
            if d.w is not None:
                add(d.w)
            for k, (v_, e_) in d.r.items():
                if (not is_dma) and e_ == eng:
                    continue
                add((k, v_, e_))
        waits = []
        seen = self.seen[eng]
        for k, v_ in need.items():
            if seen.get(k, 0) >= v_:
                continue
            seen[k] = v_
            waits.append((k, v_))
        return waits

    def _mark(self, tok, reads, writes):
        k, v_, e_ = tok
        for d in reads:
            d.r[k] = (v_, e_)
        for d in writes:
            d.w = tok
            d.r = {}

    def op(self, eng, fn, reads, writes):
        reads = _deps_of(reads)
        writes = _deps_of(writes)
        waits = self._waits(eng, reads, writes, False)
        self.cnt[eng] += 1
        tok = (self.esem[eng], self.cnt[eng], eng)
        self._mark(tok, reads, writes)
        self.stream[eng].append((waits, fn, self.esem[eng], 1))
        self.nops += 1

    def dma(self, q, out, in_):
        reads = [in_.dep]
        writes = [out.dep]
        waits = self._waits(q, reads, writes, True)
        ring = self.ring[q]
        sem = ring[self.rpos[q]]
        self.rpos[q] = (self.rpos[q] + 1) % len(ring)
        use = self.ruse[sem]
        if use > 0 and self.seen[q].get(sem, 0) < 16 * use:
            self.seen[q][sem] = 16 * use
            waits.append((sem, 16 * use))
        self.ruse[sem] = use + 1
        tok = (sem, 16 * (use + 1), None)
        self._mark(tok, reads, writes)
        oap, iap = out.ap, in_.ap
        self.stream[q].append((waits, lambda e: e.dma_start(out=oap, in_=iap), sem, 16))
        self.nops += 1

    def barrier(self):
        toks = [(self.esem[e], self.cnt[e]) for e in self.CE if self.cnt[e] > 0]
        toks += [(s_, 16 * u) for s_, u in self.ruse.items() if u > 0]
        for e in self.ALLE:
            waits = []
            for k, v_ in toks:
                if self.seen[e].get(k, 0) < v_:
                    self.seen[e][k] = v_
                    waits.append((k, v_))
            if waits:
                self.stream[e].append((waits, None, None, 0))

    def flush(self):
        nc = self.nc
        with nc.Block() as block:
            regs = {"pe": block.tensor, "act": block.scalar, "dve": block.vector,
                    "pool": block.gpsimd, "sp": block.sync}
            for e in self.ALLE:
                ops = self.stream[e]
                self.stream[e] = []
                if not ops:
                    continue

                def body(engine, ops=ops):
                    for waits, fn, sem, inc in ops:
                        for k, v_ in waits:
                            engine.wait_ge(k, v_)
                        if fn is not None:
                            fn(engine).then_inc(sem, inc)

                regs[e](body)

    def act(self, out, in_, func, bias=None, scale=None, accum_out=None):
        kw = {}
        if bias is not None:
            kw["bias"] = _ap(bias)
        if scale is not None:
            kw["scale"] = _ap(scale)
        if accum_out is not None:
            kw["accum_out"] = _ap(accum_out)
        o, i = out.ap, in_.ap
        self.op("act", lambda e: e.activation(out=o, in_=i, func=func, **kw),
                [in_, bias, scale], [out, accum_out])

    def tt(self, eng, out, in0, in1, op, also=()):
        o, a, b = out.ap, in0.ap, in1.ap
        self.op(eng, lambda e: e.tensor_tensor(out=o, in0=a, in1=b, op=op), [in0, in1] + list(also), [out])

    def ts(self, eng, out, in0, s1, op0, s2=None, op1=None):
        o, a = out.ap, in0.ap
        s1a, s2a = _ap(s1), _ap(s2)
        if op1 is None:
            self.op(eng, lambda e: e.tensor_scalar(out=o, in0=a, scalar1=s1a, scalar2=None, op0=op0),
                    [in0, s1], [out])
        else:
            self.op(eng, lambda e: e.tensor_scalar(out=o, in0=a, scalar1=s1a, scalar2=s2a, op0=op0, op1=op1),
                    [in0, s1, s2], [out])

    def stt(self, out, in0, scalar, in1, op0, op1):
        o, a, b, s_ = out.ap, in0.ap, in1.ap, _ap(scalar)
        self.op("dve", lambda e: e.scalar_tensor_tensor(out=o, in0=a, scalar=s_, in1=b, op0=op0, op1=op1),
                [in0, scalar, in1], [out])

    def copy(self, eng, out, in_):
        o, i = out.ap, in_.ap
        if eng == "act":
            self.op("act", lambda e: e.activation(out=o, in_=i, func=AF.Copy), [in_], [out])
        else:
            self.op(eng, lambda e: e.tensor_copy(out=o, in_=i), [in_], [out])

    def recip(self, out, in_):
        o, i = out.ap, in_.ap
        self.op("dve", lambda e: e.reciprocal(out=o, in_=i), [in_], [out])

    def memset(self, eng, out, val):
        o = out.ap
        self.op(eng, lambda e: e.memset(o, val), [], [out])

    def mm(self, items, reads, writes):
        def fn(e, items=items):
            ins = None
            for o, l, r, st, sp_ in items:
                ins = e.matmul(o, lhsT=l, rhs=r, start=st, stop=sp_)
            return ins
        self.op("pe", fn, reads, writes)

    def tr(self, items, ident, reads, writes):
        ia = ident.ap

        def fn(e, items=items):
            ins = None
            for o, i in items:
                ins = e.transpose(o, i, ia)
            return ins
        self.op("pe", fn, list(reads) + [ident], writes)


class Phase:
    def __init__(self, P, name):
        self.P = P
        self.nc = P.nc
        self.name = name
        self.es = ExitStack()
        self.n = 0

    def __enter__(self):
        return self

    def sb(self, shape, dt):
        self.n += 1
        h = self.es.enter_context(self.nc.sbuf_tensor(f"{self.name}_s{self.n}", list(shape), dt))
        return Tile(h)

    def ps(self, shape, dt=F32):
        self.n += 1
        h = self.es.enter_context(self.nc.psum_tensor(f"{self.name}_p{self.n}", list(shape), dt))
        return Tile(h)

    def __exit__(self, et, ev, tb):
        if et is None:
            self.P.barrier()
            self.P.flush()
        self.es.close()
        return False


class Ring:
    def __init__(self, items):
        self.items = items
        self.i = 0

    def next(self):
        x = self.items[self.i]
        self.i = (self.i + 1) % len(self.items)
        return x


def rws(r0, n, d):
    return slice(r0, r0 + d * (n - 1) + 1, d)


def host_consts():
    c = {}
    c["identf"] = np.eye(128, dtype=np.float32)
    c["ones"] = np.ones((128, 128), np.float32)
    i = np.arange(128)[:, None]
    col = np.arange(256)[None, :]
    valid = (i >= col - 128) & (i <= col)
    m_int = np.where(valid, 1.0, 0.0)
    m_first = np.where(valid & (i >= 64), 1.0, 0.0)
    m_last = np.where(valid & (i < 64), 1.0, 0.0)
    c["amask"] = np.concatenate([m_int, m_first, m_last], axis=1).astype(np.float32)
    s = np.arange(128)[:, None]
    t = np.arange(128)[None, :]
    c["m01"] = np.concatenate([(s <= t), (s >= t)], axis=1).astype(np.float32)
    c["uneg"] = np.concatenate([-(s <= t).astype(np.float32), -(s >= t).astype(np.float32),
                                -np.ones((128, 128), np.float32)], axis=1)
    return c


def rope_table(S):
    half = 8
    inv = np.float32(500000.0) ** (-np.arange(half, dtype=np.float32) / np.float32(half))
    ang = np.arange(S, dtype=np.float32)[:, None] * inv[None, :].astype(np.float32)
    ang = ang.astype(np.float32)
    return np.concatenate([np.cos(ang.astype(np.float64)), np.sin(ang.astype(np.float64))], axis=1).astype(np.float32)


def pipeline(items, prep, main, r):
    END = object()

    def drain(g):
        for _ in g:
            pass

    if not items:
        return
    drain(prep(items[0]))
    for i, it in enumerate(items):
        a = main(it)
        b = prep(items[i + 1]) if i + 1 < len(items) else None
        if b is not None and next(b, END) is END:
            b = None
        credit = 0.0
        while True:
            done_a = next(a, END) is END
            credit += r
            while b is not None and credit >= 1.0:
                credit -= 1.0
                if next(b, END) is END:
                    b = None
            if done_a:
                break
        if b is not None:
            drain(b)


def build(S0, S1, upto=99, dbg=()):
    nc = bass.Bass("TRN2", target_bir_lowering=False)
    T = S0 + S1
    NT = T // TT
    SEG = ((0, S0), (S0, S1))

    def seg_of_tile(tt):
        return 0 if tt * TT < S0 else 1

    def din(name, shape, dt=F32):
        return nc.dram_tensor(name, list(shape), dt, kind="ExternalInput").ap()

    def dscr(name, shape, dt):
        kind = "ExternalOutput" if name in dbg else "Internal"
        return nc.dram_tensor(name, list(shape), dt, kind=kind).ap()

    xin = din("xin", [T, D])
    cT_d = din("cT", [128, 16])
    rope_d = din("rope", [T, 16])
    vecs_d = din("vecs", [128, 136])
    bg_d = din("bgates", [128, 16])
    hn_d = din("hnorm", [128, D])
    identf_d = din("identf", [128, 128])
    ones_d = din("ones", [128, 128])
    amask_d = din("amask", [128, 768])
    m01_d = din("m01", [128, 256])
    uneg_d = din("uneg", [128, 384])
    adaw_d = [din("adaw0", [D, 6 * D]), din("adaw1", [D, 6 * D])]
    wqkv_d = din("wqkv", [D, 4608])
    wo_d = din("wo", [512, D])
    wgu_d = [din("wgu0", [D, 2 * FFN]), din("wgu1", [D, 2 * FFN])]
    wdn_d = [din("wdn0", [FFN, D]), din("wdn1", [FFN, D])]
    win_d = din("win", [D, 3088])
    wout_d = din("wout", [D, D])
    yp = nc.dram_tensor("yp", [S0, D], F32, kind="ExternalOutput").ap()
    ys = nc.dram_tensor("ys", [S1, D], F32, kind="ExternalOutput").ap()

    XT = dscr("XT", [D, T], F32)
    QKV = [dscr(f"QKV{g}", [T, 1544], BF16) for g in range(3)]
    OG = [dscr(f"OG{g}", [T, 520], F32) for g in range(3)]
    MT = dscr("MT", [FFN, T], BF16)
    QT1 = dscr("QT1", [512, T], BF16)
    KT1 = dscr("KT1", [512, T], BF16)
    KV1 = dscr("KV1", [T, 1540], BF16)
    SO = dscr("SO", [T, D], F32)
    GD = dscr("GD", [T, 24], F32)
    HF = dscr("HF", [T, D], F32)
    HB = dscr("HB", [T, D], F32)

    dXT = [Dep() for _ in range(NT)]
    dIN = Dep()

    P = Prog(nc)
    G = Phase(P, "glob")

    identf = G.sb([128, 128], F32)
    identb = G.sb([128, 128], BF16)
    onesb = G.sb([128, 128], BF16)
    amask = G.sb([128, 768], BF16)
    m01 = G.sb([128, 256], F32)
    uneg = G.sb([128, 384], F32)
    vecs = G.sb([128, 136], F32)
    bgt = G.sb([128, 16], F32)
    cT = G.sb([128, 8, 2], F32)
    epsc = G.sb([128, 1], F32)
    modT = [G.sb([128, 48, 2], F32) for _ in range(2)]
    acol = [[G.sb([128, 2, 8], F32) for _ in range(2)] for _ in range(2)]

    def gain(idx):
        return vecs[:, idx * 8:(idx + 1) * 8]

    def modc(l, which, s, k):
        return modT[l][:, which * 8 + k, s:s + 1]

    def s0_phase():
        with Phase(P, "s0") as ph:
            P.dma("sp", identf[:], V(identf_d, dIN))
            P.dma("pool", identb[:], V(identf_d, dIN))
            P.dma("pool", onesb[:], V(ones_d, dIN))
            P.dma("pool", amask[:], V(amask_d, dIN))
            P.dma("sp", m01[:], V(m01_d, dIN))
            P.dma("sp", uneg[:], V(uneg_d, dIN))
            P.dma("sp", vecs[:], V(vecs_d, dIN))
            P.dma("sp", bgt[:], V(bg_d, dIN))
            P.dma("sp", cT[:], V(cT_d.rearrange("p (k s) -> p k s", s=2), dIN))
            P.memset("dve", epsc[:], EPS)
            sc = ph.sb([128, 8, 2], F32)
            P.act(sc[:], cT[:], AF.Silu)
            aw = Ring([ph.sb([128, 8, 512], F32) for _ in range(3)])
            psm = ph.ps([128, 48, 2], F32)
            prow_r = Ring([ph.ps([2, 512], F32) for _ in range(2)])
            mrow_r = Ring([ph.sb([2, 512], F32) for _ in range(2)])
            for l in range(2):
                for cb in range(12):
                    a = aw.next()
                    prow = prow_r.next()
                    mrow = mrow_r.next()
                    P.dma("sp", a[:], V(adaw_d[l][:, cb * 512:(cb + 1) * 512].rearrange("(k p) f -> p k f", p=128), dIN))
                    P.mm([(prow.h[:], sc.h[:, k, :], a.h[:, k, :], k == 0, k == 7) for k in range(8)], [a, sc], [prow])
                    P.copy("act", mrow[:], prow[:])
                    P.tr([(psm.h[:, cb * 4 + jj, :], mrow.h[:, jj * 128:(jj + 1) * 128]) for jj in range(4)],
                         identf[0:2, 0:2], [mrow], [psm])
                adab = vecs[:, 40 + 48 * l: 88 + 48 * l]
                for s in range(2):
                    P.tt("dve", modT[l][:, :, s], psm[:, :, s], adab, ALU.add)
                for nrm in range(2):
                    g = gain(2 * l + nrm)
                    for s in range(2):
                        scv = modT[l][:, (1 + 3 * nrm) * 8:(2 + 3 * nrm) * 8, s]
                        P.tt("dve", acol[l][nrm][:, s, :], scv, g, ALU.mult)
                        P.tt("dve", acol[l][nrm][:, s, :], acol[l][nrm][:, s, :], g, ALU.add)


    def load_w(ph, wd, ncols, nk=NK, c0=0, k0=0):
        tiles = []
        for k in range(k0, k0 + nk):
            t = ph.sb([128, ncols], BF16)
            P.dma("pool", t[:], V(wd[k * 128:(k + 1) * 128, c0:c0 + ncols], dIN))
            tiles.append(t)
        return tiles

    class NormBufs:
        def __init__(self, ph, sq=None, ntmp=3):
            self.sq = sq if sq is not None else ph.sb([128, NK, TT], BF16)
            self.pst = ph.ps([128, TT], F32)
            self.rs = ph.sb([128, TT], F32)
            self.rinv = ph.sb([128, TT], F32)
            self.tmp_r = Ring([ph.sb([128, TT], F32) for _ in range(ntmp)])

    def norm_stats_g(nb, xT):
        P.act(V(nb.sq.h[:].rearrange("p k t -> p (k t)"), nb.sq), V(xT.h[:].rearrange("p k t -> p (k t)"), xT), AF.Square)
        yield
        P.mm([(nb.pst.h[:], onesb.h[:], nb.sq.h[:, k, :], k == 0, k == NK - 1) for k in range(NK)], [onesb, nb.sq], [nb.pst])
        yield
        P.act(nb.rs[:], nb.pst[:], AF.Sqrt, bias=epsc[:], scale=1.0 / D)
        P.recip(nb.rinv[:, 0:TT // 2], nb.rs[:, 0:TT // 2])
        yield
        P.recip(nb.rinv[:, TT // 2:TT], nb.rs[:, TT // 2:TT])
        yield

    def norm_mod_g(nb, xT, hT, l, nrm, s):
        yield from norm_stats_g(nb, xT)
        for k in range(NK):
            tmp = nb.tmp_r.next()
            P.stt(tmp[:], xT[:, k, :], acol[l][nrm][:, s, k:k + 1], nb.rinv[:], ALU.mult, ALU.mult)
            P.act(hT[:, k, :], tmp[:], AF.Identity, bias=modc(l, 3 * nrm, s, k))
            if k % 2 == 1:
                yield

    def xt_view(tt):
        return XT[:, tt * TT:(tt + 1) * TT].rearrange("(k p) t -> p k t", p=128)

    tiles = list(range(NT))

    if upto < 1:
        s0_phase()
    if upto >= 1:
        with Phase(P, "a1") as ph:
            wq = load_w(ph, wqkv_d, 4608)
            s0_phase()
            xtm_r = Ring([ph.sb([128, 4, D], F32) for _ in range(2)])
            xT_r = Ring([ph.sb([128, NK, TT], F32) for _ in range(2)])
            hT_r = Ring([ph.sb([128, NK, TT], BF16) for _ in range(2)])
            nb = NormBufs(ph)
            cs_r = Ring([ph.sb([128, 4, 16], F32) for _ in range(2)])
            rt_r = Ring([[ph.sb([128, 2, 8, 8], F32) for _ in range(4)] for _ in range(2)])
            st_r = Ring([ph.sb([128, 1544], BF16) for _ in range(3)])
            ptr = ph.ps([128, TT], F32)
            pg_r = Ring([ph.ps([128, 3, 512], F32) for _ in range(2)])
            for st in st_r.items:
                P.memset("pool", V(st.h[:, 1024:1544].rearrange("p (h c) -> p h c", c=65)[:, :, 64:65], st), 1.0)
            evac = Ring(["act", "dve"])
            ctx = {}

            def prep(tt):
                s = seg_of_tile(tt)
                xtm = xtm_r.next()
                xT = xT_r.next()
                hT = hT_r.next()
                cs = cs_r.next()
                ctx[tt] = (hT, cs, s)
                P.dma("sp", xtm[:], V(xin[tt * TT:(tt + 1) * TT, :].rearrange("(j p) d -> p j d", p=128), dIN))
                P.dma("sp", cs[:], V(rope_d[tt * TT:(tt + 1) * TT, :].rearrange("(j p) c -> p j c", p=128), dIN))
                yield
                yield
                yield
                for k in range(NK):
                    pt_ = (ptr, nb.pst)[k % 2]
                    P.tr([(pt_.h[:, j * 128:(j + 1) * 128], xtm.h[:, j, k * 128:(k + 1) * 128]) for j in range(4)],
                         identf[:], [xtm], [pt_])
                    P.copy(evac.next(), xT[:, k, :], pt_[:])
                    yield
                P.dma(STQ, V(xt_view(tt), dXT[tt]), xT[:])
                yield from norm_mod_g(nb, xT, hT, 0, 0, s)

            def main(tt):
                hT, cs, s = ctx.pop(tt)
                for j in range(4):
                    cosb = cs.h[:, j, 0:8].unsqueeze(1).unsqueeze(1).to_broadcast([128, 2, 8, 8])
                    sinb = cs.h[:, j, 8:16].unsqueeze(1).unsqueeze(1).to_broadcast([128, 2, 8, 8])
                    for g in range(3):
                        pg = pg_r.next()
                        st = st_r.next()
                        rt = rt_r.next()
                        items = []
                        for b in range(3):
                            c0 = g * 1536 + b * 512
                            for k in range(NK):
                                items.append((pg.h[:, b, :], hT.h[:, k, j * 128:(j + 1) * 128],
                                              wq[k].h[:, c0:c0 + 512], k == 0, k == NK - 1))
                        P.mm(items, [hT] + wq, [pg])
                        P.copy("act", V(st.h[:, 0:1024].rearrange("p (b f) -> p b f", b=2), st), pg[:, 0:2, :])
                        P.copy("act", V(st.h[:, 1024:1544].rearrange("p (h c) -> p h c", c=65)[:, :, 0:64], st),
                               V(pg.h[:, 2, :].rearrange("p (h c) -> p h c", c=64), pg))
                        x4 = pg.h[:, 0:2, :].rearrange("p b (h c) -> p b h c", c=64)
                        x1 = V(x4[:, :, :, 0:8], pg)
                        x2 = V(x4[:, :, :, 8:16], pg)
                        o4 = st.h[:, 0:1024].rearrange("p (b h c) -> p b h c", b=2, c=64)
                        P.tt("dve", rt[0][:], x1, V(cosb, cs), ALU.mult, also=[st])
                        P.tt("dve", rt[1][:], x2, V(sinb, cs), ALU.mult)
                        P.tt("dve", rt[2][:], x2, V(cosb, cs), ALU.mult)
                        P.tt("dve", rt[3][:], x1, V(sinb, cs), ALU.mult)
                        P.tt("dve", V(o4[:, :, :, 0:8], st), rt[0][:], rt[1][:], ALU.subtract)
                        P.tt("dve", V(o4[:, :, :, 8:16], st), rt[2][:], rt[3][:], ALU.add)
                        r0 = tt * TT + j * 128
                        P.dma(STQ, V(QKV[g][r0:r0 + 128, :], Dep()), st[:])
                        yield

            pipeline(tiles, prep, main, 1.7)

    def a2_phase():
        with Phase(P, "a2") as ph:
            PC = 8
            qtm_r = Ring([ph.sb([128, PC, 512], BF16) for _ in range(3)])
            kvtm_r = Ring([ph.sb([128, PC + 1, 1032], BF16) for _ in range(3)])
            qT_r = Ring([ph.sb([128, 4, PC * 128], BF16) for _ in range(2)])
            kT_r = Ring([ph.sb([128, 4, (PC + 1) * 128], BF16) for _ in range(2)])
            pe_r = Ring([ph.sb([128, 2, 256], BF16) for _ in range(3)])
            pt_r = Ring([ph.sb([128, 8, 256], BF16) for _ in range(4)])
            os_r = Ring([ph.sb([128, 2, 260], F32) for _ in range(3)])
            ptr_r = Ring([ph.ps([128, 4, 128], BF16) for _ in range(2)])
            sA_r = Ring([ph.ps([128, 2, 256], F32) for _ in range(2)])
            sB_r = Ring([ph.ps([128, 2, 256], F32) for _ in range(2)])
            o_r = Ring([ph.ps([128, 2, 512], F32) for _ in range(1)])
            SLOT = [4 * (h // 4) + (0, 2, 1, 3)[h % 4] for h in range(8)]
            pieces = []
            for s, (soff, S) in enumerate(SEG):
                for g, (win, d) in enumerate(GROUPS):
                    NQT = (S // d) // 128
                    for p in range(d):
                        for n0 in range(0, NQT, PC):
                            pieces.append((s, g, p, n0))
            ctx = {}

            loaded = {}
            pidx = {pc: i for i, pc in enumerate(pieces)}

            def issue_loads(pc):
                s, g, p, n0 = pc
                soff, S = SEG[s]
                d = GROUPS[g][1]
                L = S // d
                NQT = L // 128
                nq = min(PC, NQT - n0)
                qtm = qtm_r.next()
                kvtm = kvtm_r.next()
                loaded[pc] = (qtm, kvtm, nq, NQT)
                rq = soff + p + d * 128 * n0
                P.dma("sp", qtm[:, 0:nq, :],
                      V(QKV[g][rws(rq, 128 * nq, d), 0:512].rearrange("(i p) c -> p i c", p=128), dIN))
                ulo, uhi = 0, nq + 1
                if n0 == 0:
                    P.memset("pool", kvtm[0:64, 0, :], 0.0)
                    P.dma("sp", kvtm[64:128, 0, :], V(QKV[g][rws(soff + p, 64, d), 512:1544], dIN))
                    ulo = 1
                if n0 + nq == NQT:
                    P.memset("pool", kvtm[64:128, nq, :], 0.0)
                    rl = soff + p + d * (L - 64)
                    P.dma("sp", kvtm[0:64, nq, :], V(QKV[g][rws(rl, 64, d), 512:1544], dIN))
                    uhi = nq
                if uhi > ulo:
                    rk = soff + p + d * (128 * (n0 + ulo) - 64)
                    P.dma("sp", kvtm[:, ulo:uhi, :],
                          V(QKV[g][rws(rk, 128 * (uhi - ulo), d), 512:1544].rearrange("(i p) c -> p i c", p=128), dIN))

            def prep(pc):
                if pc not in loaded:
                    issue_loads(pc)
                i_ = pidx[pc]
                if i_ + 1 < len(pieces):
                    issue_loads(pieces[i_ + 1])
                qtm, kvtm, nq, NQT = loaded.pop(pc)
                qT = qT_r.next()
                kT = kT_r.next()
                ctx[pc] = (kvtm, qT, kT, nq, NQT)
                yield
                for i in range(nq):
                    pt_ = ptr_r.next()
                    P.tr([(pt_.h[:, c, :], qtm.h[:, i, c * 128:(c + 1) * 128]) for c in range(4)],
                         identb[:], [qtm], [pt_])
                    P.copy("dve", qT[:, :, i * 128:(i + 1) * 128], pt_[:])
                    yield
                for u in range(nq + 1):
                    pt_ = ptr_r.next()
                    P.tr([(pt_.h[:, c, :], kvtm.h[:, u, c * 128:(c + 1) * 128]) for c in range(4)],
                         identb[:], [kvtm], [pt_])
                    P.copy("dve", kT[:, :, u * 128:(u + 1) * 128], pt_[:])
                    yield

            def main(pc):
                s, g, p, n0 = pc
                soff, S = SEG[s]
                d = GROUPS[g][1]
                kvtm, qT, kT, nq, NQT = ctx.pop(pc)
                pts = {}
                win_lo = {}

                def do_pv(n):
                    o = o_r.next()
                    osb = os_r.next()
                    items = []
                    for h in range(8):
                        oap = o.h[:, h // 4, (h % 4) * 65:(h % 4 + 1) * 65]
                        for idx, u in enumerate((n, n + 1)):
                            off = (n - win_lo[u]) * 128
                            items.append((oap, pts[u].h[:, SLOT[h], off:off + 128],
                                          kvtm.h[:, u, 512 + h * 65:512 + (h + 1) * 65], idx == 0, idx == 1))
                    P.mm(items, [pts[n], pts[n + 1], kvtm], [o])
                    P.copy("act", osb[:], o[:, :, 0:260])
                    ro = soff + p + d * 128 * (n0 + n)
                    P.dma("sp", V(OG[g][rws(ro, 128, d), :], Dep()),
                          V(osb.h[:].rearrange("p a b -> p (a b)"), osb))

                for u in range(nq + 1):
                    ug = n0 + u
                    blo = max(ug - 1, n0)
                    bhi = min(ug, n0 + nq - 1)
                    w = (bhi - blo + 1) * 128
                    win_lo[u] = blo - n0
                    mc0 = (blo - (ug - 1)) * 128
                    var = 1 if ug == 0 else (2 if ug == NQT else 0)
                    mk = amask.h[:, var * 256 + mc0: var * 256 + mc0 + w].unsqueeze(1).to_broadcast([128, 2, w])
                    pt = pt_r.next()
                    pts[u] = pt
                    ql = (blo - n0) * 128
                    for q in range(2):
                        sa_ = sA_r.next()
                        sb_ = sB_r.next()
                        items = []
                        for ci in range(2):
                            c = 2 * q + ci
                            for hh, sx in ((0, sa_), (1, sb_)):
                                pr = slice(hh * 64, (hh + 1) * 64)
                                items.append((sx.h[:, ci, 0:w], kT.h[pr, c, u * 128:(u + 1) * 128],
                                              qT.h[pr, c, ql:ql + w], True, True))
                        P.mm(items, [kT, qT], [sa_, sb_])
                        for bi, sx in enumerate((sa_, sb_)):
                            pe_ = pe_r.next()
                            P.act(pe_[:, :, 0:w], sx[:, :, 0:w], AF.Exp, scale=0.125)
                            P.tt("dve", pt[:, 4 * q + 2 * bi:4 * q + 2 * bi + 2, 0:w], pe_[:, :, 0:w], V(mk, amask), ALU.mult)
                    if u >= 2:
                        do_pv(u - 2)
                    yield
                do_pv(nq - 1)
                yield

            pipeline(pieces, prep, main, 2.1)

    def outproj_phase(name, l, wd, nkc, mk_loader):
        with Phase(P, name) as ph:
            wt = []
            for k in range(nkc):
                t = ph.sb([128, D], BF16)
                P.dma("pool", t[:], V(wd[k * 128:(k + 1) * 128, :], dIN))
                wt.append(t)
            xT_r = Ring([ph.sb([128, NK, TT], F32) for _ in range(2)])
            ytm_r = Ring([ph.sb([128, 4, nkc * 128], BF16) for _ in range(2)])
            yT_r = Ring([ph.sb([128, nkc, TT], BF16) for _ in range(2)])
            ptr_r = Ring([ph.ps([128, 4, 128], BF16) for _ in range(2)])
            pso_r = Ring([ph.ps([128, TT], F32) for _ in range(3)])
            loader = mk_loader(ph)
            ctx = {}

            def prep(tt):
                xT = xT_r.next()
                ytm = ytm_r.next()
                yT = yT_r.next()
                ctx[tt] = (xT, yT)
                P.dma("sp", xT[:], V(xt_view(tt), dXT[tt]))
                yield from loader(tt, ytm)
                for c in range(nkc):
                    pt_ = ptr_r.next()
                    P.tr([(pt_.h[:, j, :], ytm.h[:, j, c * 128:(c + 1) * 128]) for j in range(4)],
                         identb[:], [ytm], [pt_])
                    P.copy("act", yT[:, c, :], V(pt_.h[:].rearrange("p j t -> p (j t)"), pt_))
                    yield

            def main(tt):
                s = seg_of_tile(tt)
                xT, yT = ctx.pop(tt)
                for m in range(NK):
                    pso = pso_r.next()
                    P.mm([(pso.h[:], wt[c].h[:, m * 128:(m + 1) * 128], yT.h[:, c, :], c == 0, c == nkc - 1)
                          for c in range(nkc)], [yT] + wt, [pso])
                    P.stt(xT[:, m, :], pso[:], modc(l, 2, s, m), xT[:, m, :], ALU.mult, ALU.add)
                    yield
                P.dma(STQ, V(xt_view(tt), dXT[tt]), xT[:])

            pipeline(tiles, prep, main, 2)

    def a3_loader(ph):
        og_r = [Ring([ph.sb([128, 4, 520], F32) for _ in range(2)]) for _ in range(3)]
        rden_r = Ring([ph.sb([128, 32], F32) for _ in range(2)])

        def loader(tt, ytm):
            ogs = [og_r[g].next() for g in range(3)]
            for g in range(3):
                P.dma("sp", ogs[g][:], V(OG[g][tt * TT:(tt + 1) * TT, :].rearrange("(j p) c -> p j c", p=128), dIN))
            yield
            P.tt("dve", ogs[0][:], ogs[0][:], ogs[1][:], ALU.add)
            P.tt("dve", ogs[0][:], ogs[0][:], ogs[2][:], ALU.add)
            yield
            rden = rden_r.next()
            a3 = ogs[0].h[:].rearrange("p j (h c) -> p (j h) c", c=65)
            P.recip(V(rden.h[:].unsqueeze(2), rden), V(a3[:, :, 64:65], ogs[0]))
            P.tt("dve", V(ytm.h[:].rearrange("p j (h c) -> p (j h) c", c=64), ytm), V(a3[:, :, 0:64], ogs[0]),
                 V(rden.h[:].unsqueeze(2).to_broadcast([128, 32, 64]), rden), ALU.mult)
            yield
        return loader

    def f1_phase(l, fuse, pre=None):
        fuse_a3 = fuse == "a3"
        fuse_b3 = fuse == "b3"
        with Phase(P, f"f1_{l}") as ph:
            wg = load_w(ph, wgu_d[l], 2 * FFN, nk=4)
            if pre is not None:
                pre()
            wg += load_w(ph, wgu_d[l], 2 * FFN, nk=4, k0=4)
            xT_r = Ring([ph.sb([128, NK, TT], F32) for _ in range(1 if fuse else 2)])
            hT_r = Ring([ph.sb([128, NK, TT], BF16) for _ in range(2)])
            yT8 = ph.sb([128, NK, TT], BF16) if fuse_b3 else None
            nb = NormBufs(ph, sq=yT8, ntmp=2 if fuse_b3 else 3)
            sa_r = Ring([ph.sb([128, TT], F32) for _ in range(2)])
            mt_r = Ring([ph.sb([128, TT], BF16) for _ in range(2 if fuse_b3 else 4)])
            pa_r = Ring([ph.ps([128, TT], F32) for _ in range(2)])
            pb_r = Ring([ph.ps([128, TT], F32) for _ in range(2)])
            if fuse_a3:
                nkc, wd_, yT = 4, wo_d, ph.sb([128, 4, TT], BF16)
                ogs = [ph.sb([128, 4, 520], F32) for _ in range(3)]
                rden = ph.sb([128, 32], F32)
            if fuse_b3:
                nkc, wd_, yT = 8, wout_d, yT8
                hng = ph.sb([128, D], F32)
                P.dma("sp", hng[:], V(hn_d, dIN))
                hf_r = Ring([ph.sb([128, D], F32) for _ in range(2)])
                hb_r = Ring([ph.sb([128, D], F32) for _ in range(2)])
                so_r = Ring([ph.sb([128, D], F32) for _ in range(2)])
                ss_r = Ring([ph.sb([128, 8], F32) for _ in range(3)])
                junk = ph.sb([128, 256], F32)
            if fuse:
                wo_t = []
                for k in range(nkc):
                    t = ph.sb([128, D], BF16)
                    P.dma("pool", t[:], V(wd_[k * 128:(k + 1) * 128, :], dIN))
                    wo_t.append(t)
                ytm = ph.sb([128, 4, nkc * 128], BF16)
                ptr = ph.ps([128, 4, 128], BF16)
                pso_r = Ring([ph.ps([128, TT], F32) for _ in range(2)])
            ctx = {}

            def prep(tt):
                s = seg_of_tile(tt)
                xT = xT_r.next()
                hT = hT_r.next()
                ctx[tt] = hT
                P.dma("sp", xT[:], V(xt_view(tt), dXT[tt]))
                if fuse_a3:
                    for g in range(3):
                        P.dma("sp", ogs[g][:], V(OG[g][tt * TT:(tt + 1) * TT, :].rearrange("(j p) c -> p j c", p=128), dIN))
                    yield
                    yield
                    yield
                    yield
                    for j in range(4):
                        P.tt("dve", ogs[0][:, j, :], ogs[0][:, j, :], ogs[1][:, j, :], ALU.add)
                        P.tt("dve", ogs[0][:, j, :], ogs[0][:, j, :], ogs[2][:, j, :], ALU.add)
                        yield
                        a3 = ogs[0].h[:, j, :].rearrange("p (h c) -> p h c", c=65)
                        rd = rden.h[:, j * 8:(j + 1) * 8].unsqueeze(2)
                        P.recip(V(rd, rden), V(a3[:, :, 64:65], ogs[0]))
                        P.tt("dve", V(ytm.h[:, j, :].rearrange("p (h c) -> p h c", c=64), ytm), V(a3[:, :, 0:64], ogs[0]),
                             V(rd.to_broadcast([128, 8, 64]), rden), ALU.mult)
                        yield
                if fuse_b3:
                    def ld(j):
                        hf, hb, so = hf_r.next(), hb_r.next(), so_r.next()
                        r0 = tt * TT + j * 128
                        for t_, src in ((hf, HF), (hb, HB), (so, SO)):
                            P.dma("sp", t_[:], V(src[r0:r0 + 128, :], dIN))
                        return hf, hb, so
                    pend = [ld(0), ld(1)]
                    yield
                    yield
                    for j in range(4):
                        hf, hb, so = pend.pop(0)
                        ss = ss_r.next()
                        P.tt("dve", hf[:], hf[:], hb[:], ALU.add)
                        yield
                        for h in range(4):
                            P.act(junk[:], hf[:, h * 256:(h + 1) * 256], AF.Square, accum_out=ss[:, h:h + 1])
                        P.act(ss[:, 4:8], ss[:, 0:4], AF.Sqrt, bias=epsc[:], scale=1.0 / 256)
                        P.tt("dve", so[:], so[:], hng[:], ALU.mult)
                        yield
                        P.recip(ss[:, 0:4], ss[:, 4:8])
                        for h in range(4):
                            hs_ = slice(h * 256, (h + 1) * 256)
                            P.stt(ytm[:, j, hs_], hf[:, hs_], ss[:, h:h + 1], so[:, hs_], ALU.mult, ALU.mult)
                            if h == 1:
                                yield
                        if j + 2 < 4:
                            pend.append(ld(j + 2))
                        yield
                if fuse:
                    for c in range(nkc):
                        P.tr([(ptr.h[:, j, :], ytm.h[:, j, c * 128:(c + 1) * 128]) for j in range(4)],
                             identb[:], [ytm], [ptr])
                        P.copy("act", yT[:, c, :], V(ptr.h[:].rearrange("p j t -> p (j t)"), ptr))
                        yield
                    for m in range(NK):
                        pso = pso_r.next()
                        P.mm([(pso.h[:], wo_t[c].h[:, m * 128:(m + 1) * 128], yT.h[:, c, :], c == 0, c == nkc - 1)
                              for c in range(nkc)], [yT] + wo_t, [pso])
                        P.stt(xT[:, m, :], pso[:], modc(l, 2, s, m), xT[:, m, :], ALU.mult, ALU.add)
                        yield
                    P.dma(STQ, V(xt_view(tt), dXT[tt]), xT[:])
                else:
                    yield
                    yield
                    yield
                yield from norm_mod_g(nb, xT, hT, l, 1, s)

            def main(tt):
                hT = ctx.pop(tt)
                for jf in range(NJF):
                    pa = pa_r.next()
                    pb = pb_r.next()
                    P.mm([(pa.h[:], wg[k].h[:, jf * 128:(jf + 1) * 128], hT.h[:, k, :], k == 0, k == NK - 1)
                          for k in range(NK)], [hT] + wg, [pa])
                    P.mm([(pb.h[:], wg[k].h[:, FFN + jf * 128:FFN + (jf + 1) * 128], hT.h[:, k, :], k == 0, k == NK - 1)
                          for k in range(NK)], [hT] + wg, [pb])
                    sa = sa_r.next()
                    mt = mt_r.next()
                    P.act(sa[:], pa[:], AF.Silu)
                    P.tt("dve", mt[:], sa[:], pb[:], ALU.mult)
                    P.dma(STQ, V(MT[jf * 128:(jf + 1) * 128, tt * TT:(tt + 1) * TT], Dep()), mt[:])
                    yield

            pipeline(tiles, prep, main, {None: 0.55, "a3": 1.5, "b3": 2.0}[fuse])

    def f2_phase(l, fuse_fin):
        with Phase(P, f"f2_{l}") as ph:
            wdt = []
            for jf in range(NJF):
                t = ph.sb([128, D], BF16)
                P.dma("pool", t[:], V(wdn_d[l][jf * 128:(jf + 1) * 128, :], dIN))
                wdt.append(t)
            xT_r = Ring([ph.sb([128, NK, TT], F32) for _ in range(3 if fuse_fin else 2)])
            mT_r = Ring([ph.sb([128, NJF, TT], BF16) for _ in range(2)])
            pso_r = Ring([ph.ps([128, TT], F32) for _ in range(3)])
            if fuse_fin:
                nb = NormBufs(ph)
                otm_r = Ring([ph.sb([128, 4, D], F32) for _ in range(2)])
                ptr_r = Ring([ph.ps([128, 4, 128], F32) for _ in range(2)])
                evac = Ring(["act", "dve"])
            ctx = {}
            done = {}

            def post(tt):
                s = seg_of_tile(tt)
                xT = done.pop(tt)
                otm = otm_r.next()
                yield from norm_stats_g(nb, xT)
                for k in range(NK):
                    P.stt(xT[:, k, :], xT[:, k, :], vecs[:, 32 + k:33 + k], nb.rinv[:], ALU.mult, ALU.mult)
                    if k % 2 == 1:
                        yield
                for j in range(4):
                    for kh in range(2):
                        pt_ = ptr_r.next()
                        P.tr([(pt_.h[:, kk, :], xT.h[:, kh * 4 + kk, j * 128:(j + 1) * 128]) for kk in range(4)],
                             identf[:], [xT], [pt_])
                        P.copy(evac.next(), otm[:, j, kh * 512:(kh + 1) * 512],
                               V(pt_.h[:].rearrange("p a b -> p (a b)"), pt_))
                        yield
                t0 = tt * TT
                dst = yp[t0:t0 + TT, :] if s == 0 else ys[t0 - S0:t0 - S0 + TT, :]
                P.dma(STQ, V(dst.rearrange("(j p) d -> p j d", p=128), Dep()), otm[:])

            def prep(i):
                if i < NT:
                    xT = xT_r.next()
                    mT = mT_r.next()
                    ctx[i] = (xT, mT)
                    P.dma("sp", mT[:], V(MT[:, i * TT:(i + 1) * TT].rearrange("(j p) t -> p j t", p=128), dIN))
                    P.dma("sp", xT[:], V(xt_view(i), dXT[i]))
                yield
                if fuse_fin and 0 <= i - 2 < NT:
                    yield from post(i - 2)

            def main(i):
                if i >= NT:
                    return
                s = seg_of_tile(i)
                xT, mT = ctx.pop(i)
                for m in range(NK):
                    pso = pso_r.next()
                    P.mm([(pso.h[:], wdt[jf].h[:, m * 128:(m + 1) * 128], mT.h[:, jf, :], jf == 0, jf == NJF - 1)
                          for jf in range(NJF)], [mT] + wdt, [pso])
                    P.stt(xT[:, m, :], pso[:], modc(l, 5, s, m), xT[:, m, :], ALU.mult, ALU.add)
                    yield
                if fuse_fin:
                    done[i] = xT
                else:
                    P.dma(STQ, V(xt_view(i), dXT[i]), xT[:])

            items = list(range(NT + 2)) if fuse_fin else tiles
            pipeline(items, prep, main, 2.2 if fuse_fin else 1)

    if upto >= 4:
        f1_phase(0, "a3", pre=a2_phase)
        if upto < 5:
            f2_phase(0, False)
    elif upto >= 3:
        a2_phase()
        outproj_phase("a3", 0, wo_d, 4, a3_loader)
    elif upto >= 2:
        a2_phase()

    if upto >= 5:
        with Phase(P, "b1") as ph:
            wi = load_w(ph, win_d, 3088)
            f2_phase(0, False)
            xT_r = Ring([ph.sb([128, NK, TT], F32) for _ in range(2)])
            hT_r = Ring([ph.sb([128, NK, TT], BF16) for _ in range(2)])
            nb = NormBufs(ph)
            fst_r = Ring([ph.sb([128, TT], BF16) for _ in range(4)])
            kv_r = Ring([ph.sb([128, 1540], BF16) for _ in range(3)])
            so_r = Ring([ph.sb([128, D], F32) for _ in range(2)])
            gt_r = Ring([ph.sb([128, 16], F32) for _ in range(2)])
            e_r = Ring([ph.sb([128, 8], F32) for _ in range(2)])
            l_r = Ring([ph.sb([128, 8], F32) for _ in range(2)])
            a_r = Ring([ph.sb([128, 8], F32) for _ in range(2)])
            gd_r = Ring([ph.sb([128, 4, 24], F32) for _ in range(2)])
            pf_r = Ring([ph.ps([128, TT], F32) for _ in range(1)])
            pk_r = Ring([ph.ps([128, 3, 512], F32) for _ in range(2)])
            for kv in kv_r.items:
                P.memset("pool", V(kv.h[:, 512:1540].rearrange("p (h c) -> p h c", c=257)[:, :, 256:257], kv), 1.0)
            ctx = {}

            def prep(tt):
                s = seg_of_tile(tt)
                xT = xT_r.next()
                hT = hT_r.next()
                ctx[tt] = hT
                P.dma("sp", xT[:], V(xt_view(tt), dXT[tt]))
                yield
                yield
                yield
                yield from norm_mod_g(nb, xT, hT, 1, 0, s)

            def main(tt):
                hT = ctx.pop(tt)
                gd = gd_r.next()

                def fm_step(c):
                    pf = pf_r.next()
                    fst = fst_r.next()
                    P.mm([(pf.h[:], wi[k].h[:, c * 128:(c + 1) * 128], hT.h[:, k, :], k == 0, k == NK - 1)
                          for k in range(NK)], [hT] + wi, [pf])
                    if c < 4:
                        P.copy("dve", fst[:], pf[:])
                        P.dma(STQ, V(QT1[c * 128:(c + 1) * 128, tt * TT:(tt + 1) * TT], Dep()), fst[:])
                    else:
                        P.ts("dve", fst[:], pf[:], float(128 ** -0.5), ALU.mult)
                        P.dma(STQ, V(KT1[(c - 4) * 128:(c - 3) * 128, tt * TT:(tt + 1) * TT], Dep()), fst[:])

                def kv_step(j):
                    tok = slice(j * 128, (j + 1) * 128)
                    pk = pk_r.next()
                    kv = kv_r.next()
                    items = []
                    for b, c0 in enumerate((512, 1024, 1536)):
                        for k in range(NK):
                            items.append((pk.h[:, b, :], hT.h[:, k, tok], wi[k].h[:, c0:c0 + 512], k == 0, k == NK - 1))
                    P.mm(items, [hT] + wi, [pk])
                    P.ts("dve", kv[:, 0:512], pk[:, 0, :], float(128 ** -0.5), ALU.mult)
                    P.copy("dve", V(kv.h[:, 512:1540].rearrange("p (h c) -> p h c", c=257)[:, :, 0:256], kv),
                           V(pk.h[:, 1:3, :].rearrange("p b (h c) -> p (b h) c", c=256), pk))
                    r0 = tt * TT + j * 128
                    P.dma(STQ, V(KV1[r0:r0 + 128, :], Dep()), kv[:])

                def og_step(j):
                    tok = slice(j * 128, (j + 1) * 128)
                    og = pk_r.next()
                    so = so_r.next()
                    gt = gt_r.next()
                    ee = e_r.next()
                    ll = l_r.next()
                    items = []
                    for k in range(NK):
                        items.append((og.h[:, 2, 0:16], hT.h[:, k, tok], wi[k].h[:, 3072:3088], k == 0, k == NK - 1))
                    for b, c0 in enumerate((2048, 2560)):
                        for k in range(NK):
                            items.append((og.h[:, b, :], hT.h[:, k, tok], wi[k].h[:, c0:c0 + 512], k == 0, k == NK - 1))
                    P.mm(items, [hT] + wi, [og])
                    P.tt("dve", gt[:], og[:, 2, 0:16], bgt[:], ALU.add)
                    g4 = gt.h[:].rearrange("p (d k h) -> p d k h", d=2, k=2)
                    P.act(V(ee.h[:].rearrange("p (d h) -> p d h", d=2), ee), V(g4[:, :, 1, :], gt), AF.Exp, scale=-1.0)
                    P.act(ll[:], ee[:], AF.Ln, bias=1.0)
                    P.act(so[:], V(og.h[:, 0:2, :].rearrange("p b f -> p (b f)"), og), AF.Sigmoid)
                    r0 = tt * TT + j * 128
                    P.dma(STQ, V(SO[r0:r0 + 128, :], Dep()), so[:])
                    return og, gt, ll

                def post_step(j, og, gt, ll):
                    aa = a_r.next()
                    g4 = gt.h[:].rearrange("p (d k h) -> p d k h", d=2, k=2)
                    P.mm([(og.h[:, 2, 16:20], uneg.h[:, 0:128], ll.h[:, 0:4], True, True),
                          (og.h[:, 2, 20:24], uneg.h[:, 128:256], ll.h[:, 4:8], True, True),
                          (og.h[:, 2, 24:32], uneg.h[:, 256:384], ll.h[:, 0:8], True, True)], [uneg, ll], [og])
                    P.tt("dve", V(aa.h[:].rearrange("p (d h) -> p d h", d=2), aa), V(g4[:, :, 0, :], gt),
                         V(og.h[:, 2, 16:24].rearrange("p (d h) -> p d h", d=2), og), ALU.subtract)
                    P.act(gd[:, j, 0:8], aa[:], AF.Exp)
                    P.act(gd[:, j, 8:16], og[:, 2, 16:24], AF.Exp, scale=-1.0)
                    P.act(gd[:, j, 16:24], og[:, 2, 24:32], AF.Exp)

                for j in range(4):
                    kv_step(j)
                    yield
                    st_ = og_step(j)
                    yield
                    fm_step(2 * j)
                    yield
                    fm_step(2 * j + 1)
                    yield
                    post_step(j, *st_)
                    yield
                P.dma(STQ, V(GD[tt * TT:(tt + 1) * TT, :].rearrange("(j p) c -> p j c", p=128), Dep()), gd[:])

            pipeline(tiles, prep, main, 0.6)

    if upto >= 6:
        with Phase(P, "b2") as ph:
            GC = 2
            GW = GC * 128
            chains = [(dr, s) for dr in range(2) for s in range(2)]
            qT_r = {c: Ring([ph.sb([128, 4, GW], BF16) for _ in range(2)]) for c in chains}
            kT_r = {c: Ring([ph.sb([128, 4, GW], BF16) for _ in range(2)]) for c in chains}
            kv_r = {c: Ring([ph.sb([128, GC, 1540], BF16) for _ in range(2)]) for c in chains}
            gd_r = {c: Ring([ph.sb([128, GC, 24], F32) for _ in range(3)]) for c in chains}
            chat = {c: [ph.sb([128, 257], F32) for _ in range(4)] for c in chains}
            cbf = {c: [ph.sb([128, 257], BF16) for _ in range(4)] for c in chains}
            at_r = {c: Ring([ph.sb([128, 128], BF16) for _ in range(2)]) for c in chains}
            vp_r = {c: Ring([ph.sb([128, 257], BF16) for _ in range(2)]) for c in chains}
            rr_r = {c: Ring([ph.sb([128, 2], F32) for _ in range(2)]) for c in chains}
            hst_r = {c: Ring([ph.sb([128, D], F32) for _ in range(2)]) for c in chains}
            psU = {c: ph.ps([128, 512], F32) for c in chains}
            psCc = {c: ph.ps([128, 257], F32) for c in chains}

            def chain(ck):
                dr, s = ck
                soff, S = SEG[s]
                ngrp = S // GW
                order = list(range(ngrp)) if dr == 0 else list(range(ngrp - 1, -1, -1))
                HD = HF if dr == 0 else HB

                def loads(gi):
                    c0 = soff + gi * GW
                    qT = qT_r[ck].next()
                    kT = kT_r[ck].next()
                    kv = kv_r[ck].next()
                    gd = gd_r[ck].next()
                    P.dma("sp", qT[:], V(QT1[:, c0:c0 + GW].rearrange("(h p) t -> p h t", p=128), dIN))
                    P.dma("sp", kT[:], V(KT1[:, c0:c0 + GW].rearrange("(h p) t -> p h t", p=128), dIN))
                    P.dma("sp", kv[:], V(KV1[c0:c0 + GW, :].rearrange("(j p) c -> p j c", p=128), dIN))
                    P.dma("sp", gd[:], V(GD[c0:c0 + GW, :].rearrange("(j p) c -> p j c", p=128), dIN))
                    return qT, kT, kv, gd

                for h in range(4):
                    P.memset("dve", chat[ck][h][:], 0.0)
                    P.memset("dve", cbf[ck][h][:], 0.0)
                prev_d = None
                nxt = loads(order[0])
                for i, gi in enumerate(order):
                    qT, kT, kv, gd = nxt
                    if i + 1 < len(order):
                        nxt = loads(order[i + 1])
                    c0 = soff + gi * GW
                    ch_order = range(GC) if dr == 0 else range(GC - 1, -1, -1)
                    for cc in ch_order:
                        tok = slice(cc * 128, (cc + 1) * 128)
                        r0 = c0 + cc * 128
                        hst = hst_r[ck].next()
                        for h in range(4):
                            col = dr * 4 + h
                            ea = gd[:, cc, col:col + 1]
                            emb = gd[:, cc, 8 + col:9 + col]
                            dec = gd[:, cc, 16 + col:17 + col]
                            pu = psU[ck]
                            psC = psCc[ck]
                            at = at_r[ck].next()
                            vp = vp_r[ck].next()
                            rr = rr_r[ck].next()
                            ch_, cb_ = chat[ck][h], cbf[ck][h]
                            vaug = kv.h[:, cc, 512 + h * 257:512 + (h + 1) * 257]
                            P.mm([(pu.h[:, 0:128], kT.h[:, h, tok], qT.h[:, h, tok], True, True)], [kT, qT], [pu])
                            P.act(vp[:, 0:128], kv[:, cc, h * 128:(h + 1) * 128], AF.Copy, scale=ea)
                            yield
                            P.stt(at[:], pu[:, 0:128], ea, m01[:, dr * 128:(dr + 1) * 128], ALU.mult, ALU.mult)
                            P.mm([(psC.h[:], vp.h[:, 0:128], vaug, True, True)], [kv, vp], [psC])
                            yield
                            P.mm([(pu.h[:, 128:385], at.h[:], vaug, True, False),
                                  (pu.h[:, 128:385], qT.h[:, h, tok], cb_.h[:], False, True)], [at, kv, qT, cb_], [pu])
                            if prev_d is None:
                                P.copy("dve", ch_[:], psC[:])
                            else:
                                pd = prev_d[0][:, prev_d[1], 16 + col:17 + col]
                                P.stt(ch_[:], ch_[:], pd, psC[:], ALU.mult, ALU.add)
                            yield
                            P.act(rr[:, 0:1], pu[:, 384:385], AF.Abs)
                            if h < 2:
                                P.ts("dve", cb_[:], ch_[:], dec, ALU.mult)
                            else:
                                P.act(cb_[:], ch_[:], AF.Copy, scale=dec)
                            yield
                            P.ts("dve", rr[:, 0:1], rr[:, 0:1], emb, ALU.max)
                            P.recip(rr[:, 1:2], rr[:, 0:1])
                            yield
                            P.act(hst[:, h * 256:(h + 1) * 256], pu[:, 128:384], AF.Copy, scale=rr[:, 1:2])
                            yield
                        prev_d = (gd, cc)
                        P.dma(STQ, V(HD[r0:r0 + 128, :], Dep()), hst[:])

            live = [chain(c) for c in chains]
            END = object()
            while live:
                live = [g_ for g_ in live if next(g_, END) is not END]

    def b3_loader(ph):
        hng = ph.sb([128, D], F32)
        P.dma("sp", hng[:], V(hn_d, dIN))
        hf_r = Ring([ph.sb([128, 4, D], F32) for _ in range(2)])
        hb_r = Ring([ph.sb([128, 4, D], F32) for _ in range(2)])
        so_r = Ring([ph.sb([128, 4, D], F32) for _ in range(2)])
        ss_r = Ring([ph.sb([128, 8], F32) for _ in range(3)])
        junk = ph.sb([128, 256], F32)

        def loader(tt, ytm):
            hf = hf_r.next()
            hb = hb_r.next()
            so = so_r.next()
            for t_, src in ((hf, HF), (hb, HB), (so, SO)):
                P.dma("sp", t_[:], V(src[tt * TT:(tt + 1) * TT, :].rearrange("(j p) c -> p j c", p=128), dIN))
            yield
            for j in range(4):
                ss = ss_r.next()
                P.tt("dve", hf[:, j, :], hf[:, j, :], hb[:, j, :], ALU.add)
                for h in range(4):
                    P.act(junk[:], hf[:, j, h * 256:(h + 1) * 256], AF.Square, accum_out=ss[:, h:h + 1])
                P.act(ss[:, 4:8], ss[:, 0:4], AF.Sqrt, bias=epsc[:], scale=1.0 / 256)
                P.recip(ss[:, 0:4], ss[:, 4:8])
                P.tt("dve", so[:, j, :], so[:, j, :], hng[:], ALU.mult)
                for h in range(4):
                    hs_ = slice(h * 256, (h + 1) * 256)
                    P.stt(ytm[:, j, hs_], hf[:, j, hs_], ss[:, h:h + 1], so[:, j, hs_], ALU.mult, ALU.mult)
                yield
        return loader

    if upto == 7:
        outproj_phase("b3", 1, wout_d, 8, b3_loader)
    if upto >= 8:
        f1_phase(1, "b3")
        f2_phase(1, upto >= 9)

    if upto >= 100:
        with Phase(P, "fin") as ph:
            xT_r = Ring([ph.sb([128, NK, TT], F32) for _ in range(2)])
            nb = NormBufs(ph)
            otm_r = Ring([ph.sb([128, 4, D], F32) for _ in range(2)])
            ptr_r = Ring([ph.ps([128, 4, 128], F32) for _ in range(4)])
            evac = Ring(["act", "dve"])
            ctx = {}

            def prep(tt):
                xT = xT_r.next()
                ctx[tt] = xT
                P.dma("sp", xT[:], V(xt_view(tt), dXT[tt]))
                yield
                yield from norm_stats_g(nb, xT)
                for k in range(NK):
                    P.stt(xT[:, k, :], xT[:, k, :], vecs[:, 32 + k:33 + k], nb.rinv[:], ALU.mult, ALU.mult)
                    yield

            def main(tt):
                s = seg_of_tile(tt)
                xT = ctx.pop(tt)
                otm = otm_r.next()
                for j in range(4):
                    for kh in range(2):
                        pt_ = ptr_r.next()
                        P.tr([(pt_.h[:, kk, :], xT.h[:, kh * 4 + kk, j * 128:(j + 1) * 128]) for kk in range(4)],
                             identf[:], [xT], [pt_])
                        P.copy(evac.next(), otm[:, j, kh * 512:(kh + 1) * 512],
                               V(pt_.h[:].rearrange("p a b -> p (a b)"), pt_))
                        yield
                t0 = tt * TT
                if s == 0:
                    dst = yp[t0:t0 + TT, :]
                else:
                    dst = ys[t0 - S0:t0 - S0 + TT, :]
                P.dma(STQ, V(dst.rearrange("(j p) d -> p j d", p=128), Dep()), otm[:])

            pipeline(tiles, prep, main, 1.6)

    P.barrier()
    P.flush()
    return nc, P


_CACHE = {}


def make_in_maps(inp, S0, S1, cores):
    c = host_consts()
    rope = np.concatenate([rope_table(S0), rope_table(S1)], axis=0)

    def colmajor(v):
        return np.ascontiguousarray(np.asarray(v, np.float32).reshape(-1, 128).T)

    vecs = np.concatenate([colmajor(inp["l0_norm1"]), colmajor(inp["l0_norm2"]), colmajor(inp["l1_norm1"]),
                           colmajor(inp["l1_norm2"]), colmajor(inp["final_norm"]),
                           colmajor(inp["l0_ada_b"]), colmajor(inp["l1_ada_b"])], axis=1)
    bg = np.ascontiguousarray(np.broadcast_to(np.asarray(inp["l1_mlstm_b_gates"], np.float32)[None, :], (128, 16)))
    hn = np.ascontiguousarray(np.broadcast_to(np.asarray(inp["l1_mlstm_head_norm"], np.float32)[None, :], (128, D)))
    shared = {
        "rope": rope, "vecs": np.ascontiguousarray(vecs), "bgates": bg, "hnorm": hn,
        "identf": c["identf"], "ones": c["ones"], "amask": c["amask"], "m01": c["m01"], "uneg": c["uneg"],
        "adaw0": np.asarray(inp["l0_ada_w"], np.float32), "adaw1": np.asarray(inp["l1_ada_w"], np.float32),
        "wqkv": np.asarray(inp["l0_attn_w_qkv"], np.float32), "wo": np.asarray(inp["l0_attn_w_o"], np.float32),
        "wgu0": np.asarray(inp["l0_ffn_w_gu"], np.float32), "wgu1": np.asarray(inp["l1_ffn_w_gu"], np.float32),
        "wdn0": np.asarray(inp["l0_ffn_w_down"], np.float32), "wdn1": np.asarray(inp["l1_ffn_w_down"], np.float32),
        "win": np.asarray(inp["l1_mlstm_w_in"], np.float32), "wout": np.asarray(inp["l1_mlstm_w_out"], np.float32),
    }
    maps = []
    for i in cores:
        xp = np.asarray(inp["x_prompt"][i], np.float32)[:S0]
        xs = np.asarray(inp["x_sample"][i], np.float32)[:S1]
        cp = np.asarray(inp["c_prompt"][i], np.float32)
        cs = np.asarray(inp["c_sample"][i], np.float32)
        cT = np.stack([colmajor(cp), colmajor(cs)], axis=2).reshape(128, 16)
        m = dict(shared)
        m["xin"] = np.ascontiguousarray(np.concatenate([xp, xs], axis=0))
        m["cT"] = np.ascontiguousarray(cT)
        maps.append(m)
    return maps


def kernel(**inputs):
    S0, S1 = 8192, 4096
    key = (S0, S1)
    if key not in _CACHE:
        _CACHE[key] = build(S0, S1)[0]
    nc = _CACHE[key]
    in_maps = make_in_maps(inputs, S0, S1, list(range(8)))
    res = run_bass_kernel_spmd(nc, in_maps, core_ids=list(range(8)))
    y_prompt = np.stack([np.asarray(r["yp"], np.float32) for r in res.results], axis=0)
    y_sample = np.stack([np.asarray(r["ys"], np.float32) for r in res.results], axis=0)
    return (y_prompt, y_sample)
```
